# Optimizing a Trainium2 kernel written in Bass

```python
import jax, jax.numpy as jnp
from jax import lax
import numpy as np

D_MODEL = 1024
BATCH = 1
SEQ = 16384
DEPTH = 2

ROPE_THETA = 500000.0
EPS = 1e-6
NEG = -1e30
Q_BLOCK = 128

GROUP_WIDTH = D_MODEL // 4
D_MIX = 4 * GROUP_WIDTH

MLA_HEADS = 4
MLA_NOPE = 64
MLA_ROPE = 32
MLA_QK = MLA_NOPE + MLA_ROPE
MLA_V = GROUP_WIDTH // MLA_HEADS
MLA_Q_RANK = 256
MLA_KV_RANK = 128

CONV_WIDTH = 3

SG_CHUNK = 128
SG_GROUPS = 4
SG_GDIM = GROUP_WIDTH // SG_GROUPS

MOBA_HEADS = 4
MOBA_HD = GROUP_WIDTH // MOBA_HEADS
MOBA_ROT = MOBA_HD // 4
MOBA_BLOCK = 256
MOBA_TOPK = 3

SPLIT_SIZES = (
    MLA_Q_RANK, MLA_KV_RANK, MLA_ROPE, GROUP_WIDTH,
    GROUP_WIDTH, GROUP_WIDTH, GROUP_WIDTH, GROUP_WIDTH,
    GROUP_WIDTH, GROUP_WIDTH, GROUP_WIDTH,
    GROUP_WIDTH, GROUP_WIDTH, GROUP_WIDTH, GROUP_WIDTH,
)
D_IN = 3488

kernel_name = "hybrid_parallel_mla_conv_sgmlp_moba"


def rmsnorm(x, g):
    xf = x.astype(jnp.float32)
    y = xf * lax.rsqrt(jnp.mean(xf * xf, axis=-1, keepdims=True) + EPS)
    return (y * g).astype(x.dtype)


def layernorm(x, g, b):
    xf = x.astype(jnp.float32)
    mu = jnp.mean(xf, axis=-1, keepdims=True)
    var = jnp.mean(jnp.square(xf - mu), axis=-1, keepdims=True)
    y = (xf - mu) * lax.rsqrt(var + EPS)
    return (y * g + b).astype(x.dtype)


def rope(x, pos):
    rd = x.shape[-1]
    half = rd // 2
    inv = jnp.power(ROPE_THETA, -jnp.arange(half, dtype=jnp.float32) * 2.0 / rd)
    ang = pos.astype(jnp.float32)[..., None] * inv
    cos = jnp.cos(ang)[:, :, None, :]
    sin = jnp.sin(ang)[:, :, None, :]
    xf = x.astype(jnp.float32)
    x1, x2 = xf[..., :half], xf[..., half:]
    return jnp.concatenate([x1 * cos - x2 * sin, x2 * cos + x1 * sin], axis=-1).astype(x.dtype)


def causal_attention_blocks(q, k, v, scale):
    B, S, H, D = q.shape
    nq = S // Q_BLOCK
    qb = q.reshape(B, nq, Q_BLOCK, H, D).swapaxes(0, 1)
    kpos = jnp.arange(S)

    def one(args):
        qblk, i = args
        s = jnp.einsum('bqhd,bkhd->bhqk', qblk, k, preferred_element_type=jnp.float32) * scale
        qpos = i * Q_BLOCK + jnp.arange(Q_BLOCK)
        s = jnp.where(kpos[None, :] <= qpos[:, None], s, NEG)
        p = jax.nn.softmax(s, axis=-1)
        return jnp.einsum('bhqk,bkhd->bqhd', p.astype(v.dtype), v)

    out = lax.map(one, (qb, jnp.arange(nq)))
    return out.swapaxes(0, 1).reshape(B, S, H, v.shape[-1])


def moba_attention(q, k, v, scale):
    B, S, H, D = q.shape
    nb = -(-S // MOBA_BLOCK)
    Sp = nb * MOBA_BLOCK
    kk = min(MOBA_TOPK, nb)
    pad = ((0, 0), (0, Sp - S), (0, 0), (0, 0))
    kb = jnp.pad(k, pad).reshape(B, nb, MOBA_BLOCK, H, D).transpose(0, 3, 1, 2, 4)
    vb = jnp.pad(v, pad).reshape(B, nb, MOBA_BLOCK, H, D).transpose(0, 3, 1, 2, 4)
    kmean = jnp.mean(kb.astype(jnp.float32), axis=3)
    gate = jnp.einsum('bshd,bhnd->bhsn', q.astype(jnp.float32), kmean)
    qblk = jnp.arange(S) // MOBA_BLOCK
    past = jnp.arange(nb)[None, :] < qblk[:, None]
    gate = jnp.where(past, gate, NEG)
    _, idx = lax.top_k(gate, kk)
    valid = jnp.arange(kk)[None, :] < qblk[:, None]

    nq = S // Q_BLOCK
    qc = q.reshape(B, nq, Q_BLOCK, H, D).transpose(1, 0, 3, 2, 4)
    idxc = idx.reshape(B, H, nq, Q_BLOCK, kk).transpose(2, 0, 1, 3, 4)
    validc = valid.reshape(nq, Q_BLOCK, kk)
    bi = jnp.arange(B)[:, None, None, None]
    hi = jnp.arange(H)[None, :, None, None]
    n_sel = kk * MOBA_BLOCK

    def one(args):
        qx, ix, vx, i = args
        kg = kb[bi, hi, ix]
        vg = vb[bi, hi, ix]
        s_sel = jnp.einsum('bhqd,bhqrkd->bhqrk', qx, kg, preferred_element_type=jnp.float32) * scale
        s_sel = jnp.where(vx[None, None, :, :, None], s_sel, NEG).reshape(B, H, Q_BLOCK, n_sel)
        qpos = i * Q_BLOCK + jnp.arange(Q_BLOCK)
        own = (i * Q_BLOCK) // MOBA_BLOCK
        ko = lax.dynamic_index_in_dim(kb, own, axis=2, keepdims=False)
        vo = lax.dynamic_index_in_dim(vb, own, axis=2, keepdims=False)
        s_own = jnp.einsum('bhqd,bhkd->bhqk', qx, ko, preferred_element_type=jnp.float32) * scale
        kpos = own * MOBA_BLOCK + jnp.arange(MOBA_BLOCK)
        s_own = jnp.where(kpos[None, :] <= qpos[:, None], s_own, NEG)
        p = jax.nn.softmax(jnp.concatenate([s_sel, s_own], axis=-1), axis=-1).astype(v.dtype)
        p_sel = p[..., :n_sel].reshape(B, H, Q_BLOCK, kk, MOBA_BLOCK)
        p_own = p[..., n_sel:]
        return (jnp.einsum('bhqrk,bhqrkd->bhqd', p_sel, vg)
                + jnp.einsum('bhqk,bhkd->bhqd', p_own, vo))

    out = lax.map(one, (qc, idxc, validc, jnp.arange(nq)))
    return out.transpose(1, 0, 3, 2, 4).reshape(B, S, H, D)


def short_conv(z, w):
    C = z.shape[-1]
    return lax.conv_general_dilated(
        z, w[:, None, :].astype(z.dtype), window_strides=(1,), padding=[(CONV_WIDTH - 1, 0)],
        dimension_numbers=('NWC', 'WIO', 'NWC'), feature_group_count=C)


def spatial_gate(u, v, ln_g, ln_b, w_s, b_s):
    B, S, C = v.shape
    nc = S // SG_CHUNK
    vn = layernorm(v, ln_g, ln_b).reshape(B, nc, SG_CHUNK, SG_GROUPS, SG_GDIM)
    mask = jnp.tril(jnp.ones((SG_CHUNK, SG_CHUNK), dtype=bool))
    ws = jnp.where(mask[None], w_s, jnp.zeros_like(w_s))
    mixed = jnp.einsum('gts,bcsgd->bctgd', ws, vn) + b_s.T[None, None, :, :, None]
    return u * mixed.reshape(B, S, C)


def hybrid_layer(x, positions, norm_g, w_in, mla_q_norm_g, mla_w_uq, mla_kv_norm_g, mla_w_ukv,
                 mla_q_g, mla_k_nope_g, mla_k_rope_g, conv_w, sg_ln_g, sg_ln_b, sg_w, sg_b,
                 moba_q_g, moba_k_g, w_out):
    B, S, _ = x.shape
    h = rmsnorm(x, norm_g)
    p = h @ w_in
    offs = np.cumsum(np.array(SPLIT_SIZES))[:-1].tolist()
    (cq, ckv, kr, g_a, c_h, c_b, c_c, g_b, sg_u, sg_v, g_c, mq, mk, mv, g_d) = jnp.split(p, offs, axis=-1)

    q = (rmsnorm(cq, mla_q_norm_g) @ mla_w_uq).reshape(B, S, MLA_HEADS, MLA_QK)
    kv = (rmsnorm(ckv, mla_kv_norm_g) @ mla_w_ukv).reshape(B, S, MLA_HEADS, MLA_NOPE + MLA_V)
    k_nope, v_a = kv[..., :MLA_NOPE], kv[..., MLA_NOPE:]
    q = rmsnorm(q, mla_q_g)
    q = jnp.concatenate([q[..., :MLA_NOPE], rope(q[..., MLA_NOPE:], positions)], axis=-1)
    k_nope = rmsnorm(k_nope, mla_k_nope_g)
    k_rope = rope(rmsnorm(kr, mla_k_rope_g)[:, :, None, :], positions)
    k = jnp.concatenate([k_nope, jnp.broadcast_to(k_rope, (B, S, MLA_HEADS, MLA_ROPE))], axis=-1)
    o_a = causal_attention_blocks(q, k, v_a, MLA_QK ** -0.5).reshape(B, S, GROUP_WIDTH)

    o_b = c_b * short_conv(c_c * c_h, conv_w)

    o_c = spatial_gate(sg_u, sg_v, sg_ln_g, sg_ln_b, sg_w, sg_b)

    qd = rmsnorm(mq.reshape(B, S, MOBA_HEADS, MOBA_HD), moba_q_g)
    kd = rmsnorm(mk.reshape(B, S, MOBA_HEADS, MOBA_HD), moba_k_g)
    vd = mv.reshape(B, S, MOBA_HEADS, MOBA_HD)
    qd = jnp.concatenate([rope(qd[..., :MOBA_ROT], positions), qd[..., MOBA_ROT:]], axis=-1)
    kd = jnp.concatenate([rope(kd[..., :MOBA_ROT], positions), kd[..., MOBA_ROT:]], axis=-1)
    o_d = moba_attention(qd, kd, vd, MOBA_HD ** -0.5).reshape(B, S, GROUP_WIDTH)

    y = jnp.concatenate([o_a * jax.nn.silu(g_a), o_b * jax.nn.silu(g_b),
                         o_c * jax.nn.silu(g_c), o_d * jax.nn.silu(g_d)], axis=-1)
    return y @ w_out


def setup_inputs(seed: int = 0) -> dict:
    key = jax.random.key(seed)
    ks = jax.random.split(key, 24)
    L, D, GW = DEPTH, D_MODEL, GROUP_WIDTH
    f32 = jnp.float32

    def nrm(k, shape, scale):
        return jax.random.normal(k, shape, f32) * scale

    def gain(k, shape):
        return 1.0 + 0.05 * jax.random.normal(k, shape, f32)

    x = jax.random.normal(ks[0], (BATCH, SEQ, D), f32)
    offset = jax.random.randint(ks[1], (BATCH, 1), 0, 4096, dtype=jnp.int32)
    positions = (offset + jnp.arange(SEQ, dtype=jnp.int32)[None, :]).astype(jnp.int32)
    return {
        "x": x,
        "positions": positions,
        "norm_g": gain(ks[2], (L, D)),
        "w_in": nrm(ks[3], (L, D, D_IN), D ** -0.5),
        "mla_q_norm_g": gain(ks[4], (L, MLA_Q_RANK)),
        "mla_w_uq": nrm(ks[5], (L, MLA_Q_RANK, MLA_HEADS * MLA_QK), MLA_Q_RANK ** -0.5),
        "mla_kv_norm_g": gain(ks[6], (L, MLA_KV_RANK)),
        "mla_w_ukv": nrm(ks[7], (L, MLA_KV_RANK, MLA_HEADS * (MLA_NOPE + MLA_V)), MLA_KV_RANK ** -0.5),
        "mla_q_g": gain(ks[8], (L, MLA_QK)),
        "mla_k_nope_g": gain(ks[9], (L, MLA_NOPE)),
        "mla_k_rope_g": gain(ks[10], (L, MLA_ROPE)),
        "conv_w": nrm(ks[11], (L, CONV_WIDTH, GW), CONV_WIDTH ** -0.5),
        "sg_ln_g": gain(ks[12], (L, GW)),
        "sg_ln_b": nrm(ks[13], (L, GW), 0.02),
        "sg_w": nrm(ks[14], (L, SG_GROUPS, SG_CHUNK, SG_CHUNK), SG_CHUNK ** -0.5),
        "sg_b": 1.0 + nrm(ks[15], (L, SG_GROUPS, SG_CHUNK), 0.1),
        "moba_q_g": gain(ks[16], (L, MOBA_HD)),
        "moba_k_g": gain(ks[17], (L, MOBA_HD)),
        "w_out": nrm(ks[18], (L, D_MIX, D), D_MIX ** -0.5),
    }


def reference(x, positions, norm_g, w_in, mla_q_norm_g, mla_w_uq, mla_kv_norm_g, mla_w_ukv,
              mla_q_g, mla_k_nope_g, mla_k_rope_g, conv_w, sg_ln_g, sg_ln_b, sg_w, sg_b,
              moba_q_g, moba_k_g, w_out):
    for l in range(DEPTH):
        x = x + hybrid_layer(x, positions, norm_g[l], w_in[l], mla_q_norm_g[l], mla_w_uq[l],
                             mla_kv_norm_g[l], mla_w_ukv[l], mla_q_g[l], mla_k_nope_g[l],
                             mla_k_rope_g[l], conv_w[l], sg_ln_g[l], sg_ln_b[l], sg_w[l], sg_b[l],
                             moba_q_g[l], moba_k_g[l], w_out[l])
    return x
```

```python
import contextlib
import numpy as np
import ml_dtypes
import concourse.bass as bass
import concourse.mybir as mybir
from concourse.bass_utils import run_bass_kernel_spmd

F32 = mybir.dt.float32
BF16 = mybir.dt.bfloat16
I32 = mybir.dt.int32
ALU = mybir.AluOpType
AF = mybir.ActivationFunctionType
AX = mybir.AxisListType

ENGS = ("pe", "act", "dve", "pool", "sp")


class Tok:
    __slots__ = ("name", "w", "rs", "excl")

    def __init__(self, name="t", excl=False):
        self.name = name
        self.w = []
        self.rs = []
        self.excl = excl


class Op:
    __slots__ = ("eng", "fn", "deps", "sig", "chan", "val", "kind", "inc", "src")


class Prog:
    def __init__(self, nc):
        self.nc = nc
        self.ops = {e: [] for e in ENGS}
        self.chan_cnt = {}
        self.last_c = {}
        self.last_d = {}
        self.batch = {}
        self.prev_last = {}

    def seal(self, chan):
        b = self.batch.get(chan, [])
        if not b:
            return
        for o in b:
            o.val = b[-1].val
        self.prev_last[chan] = b[-1]
        self.batch[chan] = []

    def _add(self, eng, fn, reads, writes, kind, chan=None, inc=1, extra_deps=(), awrites=(), seal=True):
        op = Op()
        op.eng, op.fn, op.kind, op.chan, op.sig, op.val, op.inc = eng, fn, kind, chan, False, None, inc
        import sys
        f = sys._getframe(2)
        while f.f_code.co_name in ("transpose_to", "proj", "finalize", "emit_pv", "rsqrt_cols") and f.f_back is not None and False:
            f = f.f_back
        op.src = f"{f.f_code.co_name}:{f.f_lineno}"
        writes = list(writes) + [t for t in reads if t.excl]
        reads = [t for t in reads if not t.excl]
        deps = list(extra_deps)
        if kind == "dma" and not self.batch.get(chan) and chan in self.prev_last:
            deps.append(self.prev_last[chan])
        for t in reads:
            deps.extend(t.w)
        for t in writes:
            deps.extend(t.w)
            deps.extend(t.rs)
        for t in awrites:
            deps.extend(t.rs)
        seen = set()
        dl = []
        for d in deps:
            if id(d) not in seen:
                seen.add(id(d))
                dl.append(d)
        op.deps = dl
        for t in reads:
            t.rs.append(op)
        for t in writes:
            t.w = [op]
            t.rs = []
        for t in awrites:
            t.w.append(op)
        if kind == "dma":
            c = self.chan_cnt.get(chan, 0) + inc
            self.chan_cnt[chan] = c
            op.val = c
            op.sig = True
            self.last_d[chan] = op
            self.batch.setdefault(chan, []).append(op)
            if seal:
                self.seal(chan)
        elif kind == "c":
            self.last_c[eng] = op
        self.ops[eng].append(op)
        return op

    def op(self, eng, meth, kw, reads=(), writes=()):
        def fn(e):
            return getattr(e, meth)(**kw)
        return self._add(eng, fn, list(reads), list(writes), "c")

    def dma(self, chan, out, in_, reads=(), writes=(), eng="sp", awrites=(), seal=True, **kw):
        def fn(e):
            return e.dma_start(out=out, in_=in_, **kw)
        return self._add(eng, fn, list(reads), list(writes), "dma", chan=chan, inc=16, awrites=list(awrites), seal=seal)

    def coll(self, chan, fn, reads=(), writes=(), inc=1):
        return self._add("pool", fn, list(reads), list(writes), "dma", chan=chan, inc=inc)

    def barrier(self):
        deps = list(self.last_c.values()) + list(self.last_d.values())
        for e in ENGS:
            self._add(e, None, [], [], "w", extra_deps=deps)

    def emit(self, final_waits=()):
        nc = self.nc
        for ch in list(self.batch):
            self.seal(ch)
        for e in ENGS:
            for op in self.ops[e]:
                for d in op.deps:
                    if d.kind == "c" and not (d.eng == "pe" and op.eng == "pe" and op.kind == "c"):
                        d.sig = True
        for e in ENGS:
            c = 0
            for op in self.ops[e]:
                if op.kind == "c" and op.sig:
                    c += 1
                    op.val = c
        with contextlib.ExitStack() as st:
            sems = {}
            for e in ENGS:
                sems[("e", e)] = st.enter_context(nc.semaphore(f"s_{e}"))
            for ch in self.chan_cnt:
                sems[("d", ch)] = st.enter_context(nc.semaphore(f"d_{ch}"))
            block = st.enter_context(nc.Block())

            def key(d):
                return ("d", d.chan) if d.kind == "dma" else ("e", d.eng)

            def mk(e):
                ops = self.ops[e]

                def body(eng):
                    waited = {}
                    for op in ops:
                        for d in op.deps:
                            if d.kind == "w":
                                continue
                            if d.kind == "c" and d.eng == "pe" and e == "pe" and op.kind == "c":
                                continue
                            k = key(d)
                            if waited.get(k, 0) >= d.val:
                                continue
                            eng.wait_ge(sems[k], d.val)
                            waited[k] = d.val
                        if op.fn is None:
                            continue
                        try:
                            ins = op.fn(eng)
                        except Exception:
                            print("FAILED OP at", op.src, "engine", e)
                            raise
                        if op.kind == "dma":
                            ins.then_inc(sems[("d", op.chan)], op.inc)
                        elif op.sig:
                            ins.then_inc(sems[("e", e)], 1)
                    if e == "sp":
                        for op in final_waits:
                            k = key(op)
                            if waited.get(k, 0) >= op.val:
                                continue
                            eng.wait_ge(sems[k], op.val)
                            waited[k] = op.val
                return body

            block.tensor(mk("pe"))
            block.scalar(mk("act"))
            block.vector(mk("dve"))
            block.gpsimd(mk("pool"))
            block.sync(mk("sp"))
        return nc


class B:
    __slots__ = ("ap", "t")

    def __init__(self, ap, t=None):
        self.ap = ap
        self.t = t if t is not None else Tok()


NCORES = 8
SEQ = 16384
DM = 1024
TOK = 2048
DEPTH = 2
EPS = 1e-6
THETA = 500000.0
NEGB = -30000.0

OFF = {}
_o = 0
for _n, _w in (("cq", 256), ("ckv", 128), ("kr", 32), ("ga", 256), ("ch", 256), ("cb", 256), ("cc", 256),
               ("gb", 256), ("u", 256), ("sv", 256), ("gc", 256), ("mq", 256), ("mk", 256), ("mv", 256), ("gd", 256)):
    OFF[_n] = _o
    _o += _w
T_ORDER = ("cq", "ckv", "kr", "sv", "mq", "mk", "mv")
T_W = {"cq": 256, "ckv": 128, "kr": 32, "sv": 256, "mq": 256, "mk": 256, "mv": 256}
TOFF = {}
_o = 0
for _n in T_ORDER:
    TOFF[_n] = _o
    _o += T_W[_n]
NT_COLS = 1440
F_BLOCKS = (("ch", 0), ("cc", 0), ("cb", 0), ("gb", 0), ("ch", 1), ("cc", 1), ("cb", 1), ("gb", 1),
            ("u", 0), ("gc", 0), ("u", 1), ("gc", 1), ("ga", 0), ("ga", 1), ("gd", 0), ("gd", 1))
NF_COLS = 2048

ROW_KN, ROW_KR, ROW_VA, ROW_KD, ROW_VD, R_PAY = 0, 256, 288, 674, 930, 1316
VW = (65, 128, 65, 128)
VOFF = (0, 33280, 98816, 132096)
VSEC = 197632


def owned_blocks(c):
    return [8 * a + (c if a % 2 == 0 else 7 - c) for a in range(8)]


def gblock(r, b):
    return 8 * b + (r if b % 2 == 0 else 7 - r)


def build(layers=(0, 1), debug=False):
    nc = bass.Bass("TRN2", target_bir_lowering=False)
    P = Prog(nc)
    NL = len(layers)

    def din(name, shape, dt=F32):
        return nc.dram_tensor(name, list(shape), dt, kind="ExternalInput").ap()

    xT_in = din("xT", [DM, TOK])
    pos_in = din("pos", [128, 16], I32)
    ident_in = din("ident", [128, 128])
    tri_in = din("tri", [128, 2, 256])
    trilm_in = din("trilm", [128, 128])
    inv_in = din("invf", [1, 24])
    ctab_in = din("ctab", [1, 32])
    mtab_in = din("mtab", [1, 3 * 512])
    onehot_in = din("onehot", [4, 64, 4096], BF16)
    w = {}
    for li in range(NL):
        w[li] = dict(
            wT=din(f"wT{li}", [DM, NT_COLS]), wF=din(f"wF{li}", [DM, NF_COLS]),
            ng=din(f"ng{li}", [128, 8]), qng=din(f"qng{li}", [128, 2]), kvng=din(f"kvng{li}", [128, 1]),
            wuq=din(f"wuq{li}", [256, 384]), wukv=din(f"wukv{li}", [128, 512]),
            gvec=din(f"gvec{li}", [1, 832]),
            convw=din(f"convw{li}", [128, 2, 3]),
            wsT=din(f"wsT{li}", [128, 4, 128]), sgb=din(f"sgb{li}", [1, 512]),
            wout=din(f"wout{li}", [DM, DM]),
        )
    out_T = nc.dram_tensor("outT", [DM, TOK], F32, kind="ExternalOutput").ap()

    pay = [nc.dram_tensor(f"pay{m}", [R_PAY, 512], BF16) for m in range(4)]
    gat = [nc.dram_tensor(f"gat{m}", [NCORES * R_PAY, 512], BF16) for m in range(4)]
    payf = nc.dram_tensor("payf", [256, 24], F32)
    gatf = nc.dram_tensor("gatf", [NCORES * 256, 24], F32)
    qT_d = nc.dram_tensor("qT_d", [4, 96, TOK], BF16).ap()
    qa_d = nc.dram_tensor("qa_d", [4, 128, TOK], BF16).ap()
    qdf_d = nc.dram_tensor("qdf_d", [4, 64, TOK], F32).ap()
    t_pay = [Tok() for _ in range(4)]
    t_gat = [Tok() for _ in range(4)]
    t_payf, t_gatf = Tok(), Tok()
    t_qT, t_qdb, t_qdf = Tok(), Tok(), Tok()

    ARENA_W = 52736
    arena = nc.alloc_sbuf_tensor("arena", [128, ARENA_W], F32)
    apos = [0]

    def sb(shape, dt=F32):
        n = int(np.prod(shape[1:]))
        nb = n * (2 if dt == BF16 else 4)
        nw = (nb + 31) // 32 * 8
        assert apos[0] + nw <= ARENA_W, f"SBUF arena overflow {apos[0] + nw}"
        v = arena[:, apos[0]:apos[0] + nw]
        apos[0] += nw
        if dt != F32:
            v = v.bitcast(dt)
        v = v[:, 0:n]
        if len(shape) == 3:
            v = v.rearrange("p (a b) -> p a b", a=shape[1])
        elif len(shape) == 4:
            v = v.rearrange("p (a b c) -> p a b c", a=shape[1], b=shape[2])
        if shape[0] != 128:
            v = v[0:shape[0]]
        return B(v)

    banks = [nc.alloc_psum_tensor(f"bank{i}", [128, 512], F32) for i in range(8)]
    bk = [Tok(f"bank{i}", excl=True) for i in range(8)]

    def ps(bank, w0, nw, dt=F32, parts=128):
        v = banks[bank][:, w0:w0 + nw]
        if dt != F32:
            v = v.bitcast(dt)
        if parts != 128:
            v = v[0:parts]
        return v

    xT = sb([128, 8, TOK])
    YT = sb([128, 8, TOK], BF16)
    xT_t = [[Tok() for _ in range(4)] for _ in range(8)]
    YT_t = [[Tok() for _ in range(4)] for _ in range(8)]
    identf = sb([128, 128]); identb = sb([128, 128], BF16)
    onesf = sb([128, 128]); onesb = sb([128, 128], BF16)
    tri = sb([128, 2, 256], BF16)
    trilm = sb([128, 128])
    mhalf = sb([128, 32])
    halfpi = sb([128, 1])
    ctab = sb([128, 32])
    mtab = sb([128, 3, 8, 64])
    cosA = sb([128, 16, 16]); sinA = sb([128, 16, 16]); cosD = sb([128, 16, 8]); sinD = sb([128, 16, 8])
    rstdbc = sb([128, TOK]); rstd_t = [Tok() for _ in range(4)]
    ngs = sb([128, 8]); qngs = sb([128, 2]); kvngs = sb([128, 1])
    wuq = sb([128, 2, 384], BF16); wukv = sb([128, 512], BF16)
    gvec = sb([128, 832])
    convw = sb([128, 2, 3])
    wsT = sb([128, 4, 128], BF16)
    bsbc = sb([128, 2, 128])
    ztl = sb([128, 2, 8, 2]); c0s = sb([128, 2, 8, 2]); cbgs = sb([128, 2, 8, 2])
    persist_mark = apos[0]

    rr = {"i": 0}

    def alt(*engs):
        rr["i"] += 1
        return engs[rr["i"] % len(engs)]

    stg = sb([128, 1024]); stg2 = sb([128, 2048])
    P.dma("c0", identf.ap, ident_in, writes=[identf.t], seal=False)
    P.dma("c0", stg2.ap[:, 0:512].rearrange("p (a b) -> p a b", a=2), tri_in, writes=[stg2.t], seal=False)
    P.dma("c0", trilm.ap, trilm_in, writes=[trilm.t], seal=False)
    P.dma("c0", ctab.ap, ctab_in.partition_broadcast(128), writes=[ctab.t], seal=False)
    P.dma("c0", mtab.ap.rearrange("p a b c -> p (a b c)"), mtab_in.partition_broadcast(128), writes=[mtab.t], seal=False)
    P.dma("c0", stg.ap[:, 0:24], inv_in.partition_broadcast(128), writes=[stg.t], seal=False)
    P.dma("c0", stg.ap[:, 32:48].bitcast(I32), pos_in, awrites=[stg.t], seal=False)
    P.seal("c0")
    for kc in range(8):
        for tt in range(4):
            P.dma(f"xin{tt}", xT.ap[:, kc, tt * 512:(tt + 1) * 512], xT_in[kc * 128:(kc + 1) * 128, tt * 512:(tt + 1) * 512],
                  writes=[xT_t[kc][tt]], seal=False)
    for tt in range(4):
        P.seal(f"xin{tt}")
    P.op('dve', 'tensor_copy', dict(out=identb.ap, in_=identf.ap), reads=[identf.t], writes=[identb.t])
    P.op('dve', 'tensor_copy', dict(out=tri.ap, in_=stg2.ap[:, 0:512].rearrange('p (a b) -> p a b', a=2)), reads=[stg2.t], writes=[tri.t])
    P.op('pool', 'memset', dict(ap=onesf.ap, constant=1.0), writes=[onesf.t])
    P.op('pool', 'memset', dict(ap=onesb.ap, constant=1.0), writes=[onesb.t])
    P.op('pool', 'memset', dict(ap=mhalf.ap, constant=-0.5), writes=[mhalf.t])
    P.op('pool', 'memset', dict(ap=halfpi.ap, constant=float(np.pi / 2)), writes=[halfpi.t])

    posf = sb([128, 16])
    P.op('dve', 'tensor_copy', dict(out=posf.ap, in_=stg.ap[:, 32:48].bitcast(I32)), reads=[stg.t], writes=[posf.t])
    for (nf, f0, cosT, sinT) in ((16, 0, cosA, sinA), (8, 16, cosD, sinD)):
        ang = sb([128, 16, nf]); kf = sb([128, 16, nf]); ki = sb([128, 16, nf], I32)
        P.op('dve', 'tensor_tensor', dict(out=ang.ap, in0=posf.ap.unsqueeze(2).to_broadcast([128, 16, nf]), in1=stg.ap[:, f0:f0 + nf].unsqueeze(1).to_broadcast([128, 16, nf]), op=ALU.mult), reads=[posf.t, stg.t], writes=[ang.t])
        P.op('dve', 'tensor_single_scalar', dict(out=kf.ap, in_=ang.ap, scalar=float(1.0 / (2 * np.pi)), op=ALU.mult), reads=[ang.t], writes=[kf.t])
        P.op('dve', 'tensor_copy', dict(out=ki.ap, in_=kf.ap), reads=[kf.t], writes=[ki.t])
        P.op('dve', 'tensor_copy', dict(out=kf.ap, in_=ki.ap), reads=[ki.t], writes=[kf.t])
        P.op('dve', 'scalar_tensor_tensor', dict(out=ang.ap, in0=kf.ap, scalar=-6.28125, in1=ang.ap, op0=ALU.mult, op1=ALU.add), reads=[kf.t], writes=[ang.t])
        P.op('dve', 'scalar_tensor_tensor', dict(out=ang.ap, in0=kf.ap, scalar=-float(2 * np.pi - 6.28125), in1=ang.ap, op0=ALU.mult, op1=ALU.add), reads=[kf.t], writes=[ang.t])
        P.op('dve', 'tensor_single_scalar', dict(out=kf.ap, in_=ang.ap, scalar=float(np.pi), op=ALU.is_gt), reads=[ang.t], writes=[kf.t])
        P.op('dve', 'scalar_tensor_tensor', dict(out=ang.ap, in0=kf.ap, scalar=-float(2 * np.pi), in1=ang.ap, op0=ALU.mult, op1=ALU.add), reads=[kf.t], writes=[ang.t])
        P.op('act', 'activation', dict(out=sinT.ap, in_=ang.ap, func=AF.Sin), reads=[ang.t], writes=[sinT.t])
        P.op('dve', 'tensor_single_scalar', dict(out=kf.ap, in_=ang.ap, scalar=-1.0, op=ALU.mult), reads=[ang.t], writes=[kf.t])
        P.op('dve', 'tensor_tensor', dict(out=kf.ap, in0=kf.ap, in1=ang.ap, op=ALU.max), reads=[ang.t], writes=[kf.t])
        P.op('act', 'activation', dict(out=cosT.ap, in_=kf.ap, func=AF.Sin, bias=halfpi.ap[:, 0:1], scale=-1.0), reads=[kf.t, halfpi.t], writes=[cosT.t])
    P.barrier()
    apos[0] = persist_mark

    final_ops = []

    def rsqrt_cols(eng_unused, dst, src_ap, n, scale, reads, extra=None):
        P.op('dve', 'tensor_scalar', dict(out=dst.ap[:, 0:n], in0=src_ap, scalar1=float(scale), scalar2=float(EPS), op0=ALU.mult, op1=ALU.add), reads=reads, writes=[dst.t])
        P.op('pool', 'tensor_tensor', dict(out=dst.ap[:, 0:n], in0=dst.ap[:, 0:n], in1=mhalf.ap[:, 0:n], op=ALU.pow), reads=[mhalf.t], writes=[dst.t])

    def transpose_to(dst_ap, dst_toks, src_ap, src_toks, pview, ptok, ident, evac_eng):
        P.op('pe', 'transpose', dict(out=pview, in_=src_ap, identity=ident.ap), reads=list(src_toks) + [ident.t], writes=[ptok])
        if evac_eng == "act":
            P.op('act', 'copy', dict(out=dst_ap, in_=pview), reads=[ptok], writes=list(dst_toks))
        else:
            P.op(evac_eng, 'tensor_copy', dict(out=dst_ap, in_=pview), reads=[ptok], writes=list(dst_toks))

    for li in range(NL):
        W = w[li]
        last_layer = (li == NL - 1)
        mark_l = apos[0]
        wst = [sb([128, 1024]) for _ in range(2)]
        vn = [sb([128, 256], BF16) for _ in range(16)]
        P.dma("lc", ngs.ap, W["ng"], writes=[ngs.t], seal=False)
        P.dma("lc", qngs.ap, W["qng"], writes=[qngs.t], seal=False)
        P.dma("lc", kvngs.ap, W["kvng"], writes=[kvngs.t], seal=False)
        P.dma("lc", gvec.ap, W["gvec"].partition_broadcast(128), writes=[gvec.t], seal=False)
        P.dma("lc", convw.ap, W["convw"], writes=[convw.t], seal=False)
        P.seal("lc")
        for kc in range(2):
            s_ = wst[kc]
            P.dma(f"ws{kc}", s_.ap[:, 0:384], W["wuq"][kc * 128:(kc + 1) * 128, :], writes=[s_.t])
            P.op('dve', 'tensor_scalar', dict(out=wuq.ap[:, kc, :], in0=s_.ap[:, 0:384], scalar1=qngs.ap[:, kc:kc + 1], scalar2=None, op0=ALU.mult), reads=[s_.t, qngs.t], writes=[wuq.t])
        s_ = wst[0]
        P.dma("ws0", s_.ap[:, 0:512], W["wukv"], writes=[s_.t])
        P.op('dve', 'tensor_scalar', dict(out=wukv.ap, in0=s_.ap[:, 0:512], scalar1=kvngs.ap[:, 0:1], scalar2=None, op0=ALU.mult), reads=[s_.t, kvngs.t], writes=[wukv.t])
        s_ = wst[0]
        P.dma("ws0", s_.ap[:, 0:512].rearrange("p (g t) -> p g t", g=4), W["wsT"], writes=[s_.t])
        P.op('dve', 'tensor_tensor', dict(out=wsT.ap, in0=s_.ap[:, 0:512].rearrange('p (g t) -> p g t', g=4), in1=trilm.ap.unsqueeze(1).to_broadcast([128, 4, 128]), op=ALU.mult), reads=[s_.t, trilm.t], writes=[wsT.t])
        s_ = wst[1]
        P.dma("ws1", s_.ap[0:1, 0:512], W["sgb"], writes=[s_.t])
        for pr in range(2):
            tb = bk[6 + pr]
            pv = ps(6 + pr, 0, 128)
            for hf in range(2):
                g = 2 * pr + hf
                pvh = banks[6 + pr][hf * 64:(hf + 1) * 64, 0:128]
                P.op('pe', 'matmul', dict(out=pvh, lhsT=onesf.ap[0:1, 0:64], rhs=s_.ap[0:1, g * 128:(g + 1) * 128], start=True, stop=True, tile_position=(0, 64 * hf)), reads=[onesf.t, s_.t], writes=[tb])
            P.op('dve', 'tensor_copy', dict(out=bsbc.ap[:, pr, :], in_=pv), reads=[tb], writes=[bsbc.t])

        mark_T = apos[0]
        WT = sb([128, 8, NT_COLS], BF16)
        hnT = sb([128, 8, 512], BF16)
        xsq = [sb([128, 512]) for _ in range(2)]
        wi = 0
        for kc in range(8):
            for hf in range(2):
                s_ = wst[wi % 2]
                P.dma(f"ws{wi % 2}", s_.ap[:, 0:720], W["wT"][kc * 128:(kc + 1) * 128, hf * 720:(hf + 1) * 720], writes=[s_.t])
                if wi % 2 == 0:
                    P.op('dve', 'tensor_scalar', dict(out=WT.ap[:, kc, hf * 720:(hf + 1) * 720], in0=s_.ap[:, 0:720], scalar1=ngs.ap[:, kc:kc + 1], scalar2=None, op0=ALU.mult), reads=[s_.t, ngs.t], writes=[WT.t])
                else:
                    P.op('act', 'activation', dict(out=WT.ap[:, kc, hf * 720:(hf + 1) * 720], in_=s_.ap[:, 0:720], func=AF.Copy, scale=ngs.ap[:, kc:kc + 1]), reads=[s_.t, ngs.t], writes=[WT.t])
                wi += 1

        st1 = sb([128, 32]); r1 = sb([128, 32]); st2 = sb([128, 32]); r2 = sb([128, 32])
        junk = sb([128, 512])
        cqn = sb([128, 256], BF16); cqnT = sb([128, 2, 128], BF16)
        ckvn = sb([128, 128], BF16); ckvnT = sb([128, 128], BF16)
        qn = sb([128, 4, 96]); qfin = sb([128, 4, 96], BF16); ropt = sb([128, 4, 4, 16])
        kvs = sb([128, 4, 128])
        kfin = sb([128, 4, 64], BF16)
        krn = sb([128, 32]); krfin = sb([128, 32], BF16); kropt = sb([128, 4, 16])
        QTs = sb([128, 4, 128], BF16); knTs = sb([128, 2, 128], BF16); krTs = sb([32, 128], BF16)
        vAs = sb([128, 4, 386], BF16); vDs = sb([128, 4, 386], BF16)
        svs = sb([128, 256]); bnst = sb([128, 6]); bnag = sb([128, 2])
        qd = sb([128, 4, 64]); kd = sb([128, 4, 64]); kdb = sb([128, 256], BF16); dropt = sb([128, 4, 4, 8])
        kdTs = sb([128, 2, 128], BF16); qdfs = sb([64, 4, 128]); qdbs = sb([64, 4, 128], BF16)
        ksum = sb([128, 2, 8])
        tp_t = [bk[6], bk[7], bk[6], bk[7]]
        tq_t = [bk[6], bk[7]]
        pT_t = [bk[0], bk[1], bk[2]]
        q_t, kv_t, ssq_t, ks_t = bk[3], bk[4], bk[5], bk[5]
        P.op('pool', 'memset', dict(ap=vAs.ap, constant=0.0), writes=[vAs.t])
        P.op('pool', 'memset', dict(ap=vDs.ap, constant=0.0), writes=[vDs.t])
        for vs_ in (vAs, vDs):
            for (c0, wd) in ((64, 1), (65, 1), (257, 1), (258, 1)):
                P.op('pool', 'memset', dict(ap=vs_.ap[:, :, c0:c0 + 1], constant=1.0), writes=[vs_.t])
        P.op('pool', 'memset', dict(ap=junk.ap, constant=0.0), writes=[junk.t])

        gq_ap = gvec.ap[:, 0:96]; gkn_ap = gvec.ap[:, 96:160]; gkr_ap = gvec.ap[:, 160:192]
        gmq_ap = gvec.ap[:, 192:256]; gmk_ap = gvec.ap[:, 256:320]; lng_ap = gvec.ap[:, 320:576]; lnb_ap = gvec.ap[:, 576:832]

        for tt in range(4):
            tsl = slice(tt * 512, (tt + 1) * 512)
            pss = ps(5, 0, 512)
            for kc in range(8):
                xs_ = xsq[kc % 2]
                eng = "act" if kc % 2 == 0 else "pool"
                if eng == "act":
                    P.op('act', 'activation', dict(out=xs_.ap, in_=xT.ap[:, kc, tsl], func=AF.Square), reads=[xT_t[kc][tt]], writes=[xs_.t])
                else:
                    P.op('pool', 'tensor_tensor', dict(out=xs_.ap, in0=xT.ap[:, kc, tsl], in1=xT.ap[:, kc, tsl], op=ALU.mult), reads=[xT_t[kc][tt]], writes=[xs_.t])
                P.op('pe', 'matmul', dict(out=pss, lhsT=onesf.ap, rhs=xs_.ap, start=kc == 0, stop=kc == 7), reads=[xs_.t, onesf.t], writes=[ssq_t])
            P.op('dve', 'tensor_scalar', dict(out=rstdbc.ap[:, tsl], in0=pss, scalar1=float(1.0 / DM), scalar2=float(EPS), op0=ALU.mult, op1=ALU.add), reads=[ssq_t], writes=[rstd_t[tt]])
            for q4 in range(4):
                P.op('pool', 'tensor_tensor', dict(out=rstdbc.ap[:, tt * 512 + q4 * 128:tt * 512 + (q4 + 1) * 128], in0=rstdbc.ap[:, tt * 512 + q4 * 128:tt * 512 + (q4 + 1) * 128], in1=mhalf.ap[:, 0:1].to_broadcast([128, 128]), op=ALU.pow), reads=[mhalf.t], writes=[rstd_t[tt]])
            for kc in range(8):
                P.op(alt('dve', 'pool'), 'tensor_tensor', dict(out=hnT.ap[:, kc, :], in0=xT.ap[:, kc, tsl], in1=rstdbc.ap[:, tsl], op=ALU.mult), reads=[xT_t[kc][tt], rstd_t[tt]], writes=[hnT.t])

            for sub in range(4):
                tl = tt * 4 + sub
                ssl = slice(sub * 128, (sub + 1) * 128)
                gsl = slice(tl * 128, (tl + 1) * 128)
                pT = (ps(0, 0, 416), ps(1, 0, 512), ps(2, 0, 512))
                cols = ((0, 416), (416, 928), (928, 1440))
                for bi in range(3):
                    for kc in range(8):
                        P.op('pe', 'matmul', dict(out=pT[bi], lhsT=hnT.ap[:, kc, ssl], rhs=WT.ap[:, kc, cols[bi][0]:cols[bi][1]], start=kc == 0, stop=kc == 7), reads=[hnT.t, WT.t], writes=[pT_t[bi]])
                cq_ps = pT[0][:, 0:256]; ckv_ps = pT[0][:, 256:384]; kr_ps = pT[0][:, 384:416]
                sv_ps = pT[1][:, 0:256]; mq_ps = pT[1][:, 256:512]
                mk_ps = pT[2][:, 0:256]; mv_ps = pT[2][:, 256:512]
                P.op('pool', 'memset', dict(ap=st1.ap, constant=0.0), writes=[st1.t])
                P.op('act', 'activation', dict(out=junk.ap[:, 0:256], in_=cq_ps, func=AF.Square, accum_out=st1.ap[:, 0:1]), reads=[pT_t[0]], writes=[st1.t, junk.t])
                P.op('act', 'activation', dict(out=junk.ap[:, 0:128], in_=ckv_ps, func=AF.Square, accum_out=st1.ap[:, 1:2]), reads=[pT_t[0]], writes=[st1.t, junk.t])
                P.op('act', 'activation', dict(out=junk.ap[:, 0:32], in_=kr_ps, func=AF.Square, accum_out=st1.ap[:, 2:3]), reads=[pT_t[0]], writes=[st1.t, junk.t])
                P.op('act', 'activation', dict(out=junk.ap[:, 0:256], in_=mq_ps, func=AF.Square), reads=[pT_t[1]], writes=[junk.t])
                P.op('dve', 'tensor_reduce', dict(out=st1.ap[:, 4:8], in_=junk.ap[:, 0:256].rearrange('p (h d) -> p h d', h=4), axis=AX.X, op=ALU.add), reads=[junk.t], writes=[st1.t])
                P.op('act', 'activation', dict(out=junk.ap[:, 256:512], in_=mk_ps, func=AF.Square), reads=[pT_t[2]], writes=[junk.t])
                P.op('dve', 'tensor_reduce', dict(out=st1.ap[:, 8:12], in_=junk.ap[:, 256:512].rearrange('p (h d) -> p h d', h=4), axis=AX.X, op=ALU.add), reads=[junk.t], writes=[st1.t])
                P.op('dve', 'tensor_scalar', dict(out=r1.ap[:, 0:1], in0=st1.ap[:, 0:1], scalar1=1.0 / 256, scalar2=float(EPS), op0=ALU.mult, op1=ALU.add), reads=[st1.t], writes=[r1.t])
                P.op('dve', 'tensor_scalar', dict(out=r1.ap[:, 1:2], in0=st1.ap[:, 1:2], scalar1=1.0 / 128, scalar2=float(EPS), op0=ALU.mult, op1=ALU.add), reads=[st1.t], writes=[r1.t])
                P.op('dve', 'tensor_scalar', dict(out=r1.ap[:, 2:3], in0=st1.ap[:, 2:3], scalar1=1.0 / 32, scalar2=float(EPS), op0=ALU.mult, op1=ALU.add), reads=[st1.t], writes=[r1.t])
                P.op('dve', 'tensor_scalar', dict(out=r1.ap[:, 4:12], in0=st1.ap[:, 4:12], scalar1=1.0 / 64, scalar2=float(EPS), op0=ALU.mult, op1=ALU.add), reads=[st1.t], writes=[r1.t])
                P.op('dve', 'bn_stats', dict(out=bnst.ap, in_=sv_ps), reads=[pT_t[1]], writes=[bnst.t])
                P.op('dve', 'bn_aggr', dict(out=bnag.ap, in_=bnst.ap), reads=[bnst.t], writes=[bnag.t])
                P.op('dve', 'tensor_scalar', dict(out=r1.ap[:, 3:4], in0=bnag.ap[:, 1:2], scalar1=1.0, scalar2=float(EPS), op0=ALU.mult, op1=ALU.add), reads=[bnag.t], writes=[r1.t])
                P.op('pool', 'tensor_tensor', dict(out=r1.ap[:, 0:12], in0=r1.ap[:, 0:12], in1=mhalf.ap[:, 0:12], op=ALU.pow), reads=[mhalf.t], writes=[r1.t])

                P.op('act', 'activation', dict(out=cqn.ap, in_=cq_ps, func=AF.Copy, scale=r1.ap[:, 0:1]), reads=[pT_t[0], r1.t], writes=[cqn.t])
                for kc in range(2):
                    tpv = ps(6 + kc, 0, 64, BF16)
                    transpose_to(cqnT.ap[:, kc, :], [cqnT.t], cqn.ap[:, kc * 128:(kc + 1) * 128], [cqn.t], tpv, tp_t[kc], identb, "dve")
                q_ps = ps(3, 0, 384)
                for kc in range(2):
                    P.op('pe', 'matmul', dict(out=q_ps, lhsT=cqnT.ap[:, kc, :], rhs=wuq.ap[:, kc, :], start=kc == 0, stop=kc == 1), reads=[cqnT.t, wuq.t], writes=[q_t])
                P.op('act', 'activation', dict(out=ckvn.ap, in_=ckv_ps, func=AF.Copy, scale=r1.ap[:, 1:2]), reads=[pT_t[0], r1.t], writes=[ckvn.t])
                tpv = ps(6, 0, 64, BF16)
                transpose_to(ckvnT.ap, [ckvnT.t], ckvn.ap, [ckvn.t], tpv, tp_t[2], identb, "dve")
                kv_ps = ps(4, 0, 512)
                P.op('pe', 'matmul', dict(out=kv_ps, lhsT=ckvnT.ap, rhs=wukv.ap, start=True, stop=True), reads=[ckvnT.t, wukv.t], writes=[kv_t])
                P.op('act', 'activation', dict(out=junk.ap[:, 0:384], in_=q_ps, func=AF.Square), reads=[q_t], writes=[junk.t])
                P.op('dve', 'tensor_reduce', dict(out=st2.ap[:, 0:4], in_=junk.ap[:, 0:384].rearrange('p (h d) -> p h d', h=4), axis=AX.X, op=ALU.add), reads=[junk.t], writes=[st2.t])
                P.op('act', 'copy', dict(out=kvs.ap.rearrange('p h d -> p (h d)'), in_=kv_ps), reads=[kv_t], writes=[kvs.t])
                P.op('pool', 'tensor_tensor', dict(out=junk.ap[:, 0:256].rearrange('p (h d) -> p h d', h=4), in0=kvs.ap[:, :, 0:64], in1=kvs.ap[:, :, 0:64], op=ALU.mult), reads=[kvs.t], writes=[junk.t])
                P.op('dve', 'tensor_reduce', dict(out=st2.ap[:, 4:8], in_=junk.ap[:, 0:256].rearrange('p (h d) -> p h d', h=4), axis=AX.X, op=ALU.add), reads=[junk.t], writes=[st2.t])
                P.op('dve', 'tensor_scalar', dict(out=r2.ap[:, 0:4], in0=st2.ap[:, 0:4], scalar1=1.0 / 96, scalar2=float(EPS), op0=ALU.mult, op1=ALU.add), reads=[st2.t], writes=[r2.t])
                P.op('dve', 'tensor_scalar', dict(out=r2.ap[:, 4:8], in0=st2.ap[:, 4:8], scalar1=1.0 / 64, scalar2=float(EPS), op0=ALU.mult, op1=ALU.add), reads=[st2.t], writes=[r2.t])
                P.op('pool', 'tensor_tensor', dict(out=r2.ap[:, 0:8], in0=r2.ap[:, 0:8], in1=mhalf.ap[:, 0:8], op=ALU.pow), reads=[mhalf.t], writes=[r2.t])
                P.op('dve', 'tensor_tensor', dict(out=qn.ap, in0=q_ps.rearrange('p (h d) -> p h d', h=4), in1=r2.ap[:, 0:4].unsqueeze(2).to_broadcast([128, 4, 96]), op=ALU.mult), reads=[q_t, r2.t], writes=[qn.t])
                P.op('pool', 'tensor_tensor', dict(out=qn.ap, in0=qn.ap, in1=gq_ap.unsqueeze(1).to_broadcast([128, 4, 96]), op=ALU.mult), reads=[gvec.t], writes=[qn.t])
                cA = cosA.ap[:, tl, :].unsqueeze(1).to_broadcast([128, 4, 16]); sA = sinA.ap[:, tl, :].unsqueeze(1).to_broadcast([128, 4, 16])
                x1 = qn.ap[:, :, 64:80]; x2 = qn.ap[:, :, 80:96]
                P.op('pool', 'tensor_tensor', dict(out=ropt.ap[:, 0], in0=x1, in1=cA, op=ALU.mult), reads=[qn.t, cosA.t], writes=[ropt.t])
                P.op('pool', 'tensor_tensor', dict(out=ropt.ap[:, 1], in0=x2, in1=sA, op=ALU.mult), reads=[qn.t, sinA.t], writes=[ropt.t])
                P.op('pool', 'tensor_tensor', dict(out=ropt.ap[:, 2], in0=x2, in1=cA, op=ALU.mult), reads=[qn.t], writes=[ropt.t])
                P.op('pool', 'tensor_tensor', dict(out=ropt.ap[:, 3], in0=x1, in1=sA, op=ALU.mult), reads=[qn.t], writes=[ropt.t])
                P.op('pool', 'tensor_tensor', dict(out=qfin.ap[:, :, 64:80], in0=ropt.ap[:, 0], in1=ropt.ap[:, 1], op=ALU.subtract), reads=[ropt.t], writes=[qfin.t])
                P.op('pool', 'tensor_tensor', dict(out=qfin.ap[:, :, 80:96], in0=ropt.ap[:, 2], in1=ropt.ap[:, 3], op=ALU.add), reads=[ropt.t], writes=[qfin.t])
                P.op('pool', 'tensor_copy', dict(out=qfin.ap[:, :, 0:64], in_=qn.ap[:, :, 0:64]), reads=[qn.t], writes=[qfin.t])
                for h in range(4):
                    tpv = ps(6 + h % 2, 0, 64, BF16, parts=96)
                    transpose_to(QTs.ap[0:96, h, :], [QTs.t], qfin.ap[:, h, :], [qfin.t], tpv, tp_t[h % 2], identb, alt("dve", "act"))
                P.dma("s_q", qT_d[:, :, gsl].rearrange("h d t -> d h t"), QTs.ap[0:96], reads=[QTs.t], awrites=[t_qT])
                P.op('dve', 'tensor_tensor', dict(out=junk.ap[:, 0:256].rearrange('p (h d) -> p h d', h=4), in0=kvs.ap[:, :, 0:64], in1=r2.ap[:, 4:8].unsqueeze(2).to_broadcast([128, 4, 64]), op=ALU.mult), reads=[kvs.t, r2.t], writes=[junk.t])
                P.op('pool', 'tensor_tensor', dict(out=kfin.ap, in0=junk.ap[:, 0:256].rearrange('p (h d) -> p h d', h=4), in1=gkn_ap.unsqueeze(1).to_broadcast([128, 4, 64]), op=ALU.mult), reads=[junk.t, gvec.t], writes=[kfin.t])
                for hp in range(2):
                    base = hp * 193
                    P.op('pool', 'tensor_copy', dict(out=vAs.ap[:, sub, base:base + 64], in_=kvs.ap[:, 2 * hp, 64:128]), reads=[kvs.t], writes=[vAs.t])
                    P.op('pool', 'tensor_copy', dict(out=vAs.ap[:, sub, base + 129:base + 193], in_=kvs.ap[:, 2 * hp + 1, 64:128]), reads=[kvs.t], writes=[vAs.t])
                for hp in range(2):
                    tpv = ps(6 + hp, 0, 64, BF16)
                    transpose_to(knTs.ap[:, hp, :], [knTs.t], kfin.ap[:, 2 * hp:2 * hp + 2, :].rearrange("p h d -> p (h d)"), [kfin.t], tpv, tp_t[2 + hp], identb, alt("dve", "act"))
                P.op('act', 'activation', dict(out=krn.ap, in_=kr_ps, func=AF.Copy, scale=r1.ap[:, 2:3]), reads=[pT_t[0], r1.t], writes=[krn.t])
                P.op('pool', 'tensor_tensor', dict(out=krn.ap, in0=krn.ap, in1=gkr_ap, op=ALU.mult), reads=[gvec.t], writes=[krn.t])
                c1 = cosA.ap[:, tl, :]; s1 = sinA.ap[:, tl, :]
                P.op('pool', 'tensor_tensor', dict(out=kropt.ap[:, 0], in0=krn.ap[:, 0:16], in1=c1, op=ALU.mult), reads=[krn.t, cosA.t], writes=[kropt.t])
                P.op('pool', 'tensor_tensor', dict(out=kropt.ap[:, 1], in0=krn.ap[:, 16:32], in1=s1, op=ALU.mult), reads=[krn.t, sinA.t], writes=[kropt.t])
                P.op('pool', 'tensor_tensor', dict(out=kropt.ap[:, 2], in0=krn.ap[:, 16:32], in1=c1, op=ALU.mult), reads=[krn.t], writes=[kropt.t])
                P.op('pool', 'tensor_tensor', dict(out=kropt.ap[:, 3], in0=krn.ap[:, 0:16], in1=s1, op=ALU.mult), reads=[krn.t], writes=[kropt.t])
                P.op('pool', 'tensor_tensor', dict(out=krfin.ap[:, 0:16], in0=kropt.ap[:, 0], in1=kropt.ap[:, 1], op=ALU.subtract), reads=[kropt.t], writes=[krfin.t])
                P.op('pool', 'tensor_tensor', dict(out=krfin.ap[:, 16:32], in0=kropt.ap[:, 2], in1=kropt.ap[:, 3], op=ALU.add), reads=[kropt.t], writes=[krfin.t])
                tpv = ps(6, 0, 64, BF16, parts=32)
                transpose_to(krTs.ap, [krTs.t], krfin.ap, [krfin.t], tpv, tp_t[0], identb, "dve")
                P.dma("s_kn", pay[tt].ap()[ROW_KN:ROW_KN + 256, ssl].rearrange("(a p) t -> p a t", a=2), knTs.ap, reads=[knTs.t], awrites=[t_pay[tt]])
                P.dma("s_kr", pay[tt].ap()[ROW_KR:ROW_KR + 32, ssl], krTs.ap, reads=[krTs.t], awrites=[t_pay[tt]])

                P.op('dve', 'tensor_scalar', dict(out=svs.ap, in0=sv_ps, scalar1=bnag.ap[:, 0:1], scalar2=r1.ap[:, 3:4], op0=ALU.subtract, op1=ALU.mult), reads=[pT_t[1], bnag.t, r1.t], writes=[svs.t])
                P.op('pool', 'tensor_tensor', dict(out=svs.ap, in0=svs.ap, in1=lng_ap, op=ALU.mult), reads=[gvec.t], writes=[svs.t])
                P.op('pool', 'tensor_tensor', dict(out=vn[tl].ap, in0=svs.ap, in1=lnb_ap, op=ALU.add), reads=[svs.t, gvec.t], writes=[vn[tl].t])

                for (src_ps, src_tok, dst, rc, g_ap) in ((mq_ps, pT_t[1], qd, 4, gmq_ap), (mk_ps, pT_t[2], kd, 8, gmk_ap)):
                    P.op('dve', 'tensor_tensor', dict(out=dst.ap, in0=src_ps.rearrange('p (h d) -> p h d', h=4), in1=r1.ap[:, rc:rc + 4].unsqueeze(2).to_broadcast([128, 4, 64]), op=ALU.mult), reads=[src_tok, r1.t], writes=[dst.t])
                    P.op('pool', 'tensor_tensor', dict(out=dst.ap, in0=dst.ap, in1=g_ap.unsqueeze(1).to_broadcast([128, 4, 64]), op=ALU.mult), reads=[gvec.t], writes=[dst.t])
                    cD = cosD.ap[:, tl, :].unsqueeze(1).to_broadcast([128, 4, 8]); sD = sinD.ap[:, tl, :].unsqueeze(1).to_broadcast([128, 4, 8])
                    y1 = dst.ap[:, :, 0:8]; y2 = dst.ap[:, :, 8:16]
                    P.op('pool', 'tensor_tensor', dict(out=dropt.ap[:, 0], in0=y1, in1=cD, op=ALU.mult), reads=[dst.t, cosD.t], writes=[dropt.t])
                    P.op('pool', 'tensor_tensor', dict(out=dropt.ap[:, 1], in0=y2, in1=sD, op=ALU.mult), reads=[dst.t, sinD.t], writes=[dropt.t])
                    P.op('pool', 'tensor_tensor', dict(out=dropt.ap[:, 2], in0=y2, in1=cD, op=ALU.mult), reads=[dst.t], writes=[dropt.t])
                    P.op('pool', 'tensor_tensor', dict(out=dropt.ap[:, 3], in0=y1, in1=sD, op=ALU.mult), reads=[dst.t], writes=[dropt.t])
                    P.op('pool', 'tensor_tensor', dict(out=y1, in0=dropt.ap[:, 0], in1=dropt.ap[:, 1], op=ALU.subtract), reads=[dropt.t], writes=[dst.t])
                    P.op('pool', 'tensor_tensor', dict(out=y2, in0=dropt.ap[:, 2], in1=dropt.ap[:, 3], op=ALU.add), reads=[dropt.t], writes=[dst.t])
                for h in range(4):
                    tqv = banks[6 + h % 2][0:64, 0:128]
                    P.op('pe', 'transpose', dict(out=tqv, in_=qd.ap[:, h, :], identity=identf.ap), reads=[qd.t, identf.t], writes=[tq_t[h % 2]])
                    P.op('act', 'copy', dict(out=qdfs.ap[:, h, :], in_=tqv), reads=[tq_t[h % 2]], writes=[qdfs.t])
                    P.op('dve', 'tensor_copy', dict(out=qdbs.ap[:, h, :], in_=tqv), reads=[tq_t[h % 2]], writes=[qdbs.t])
                P.dma("s_qf", qdf_d[:, :, gsl].rearrange("h d t -> d h t"), qdfs.ap, reads=[qdfs.t], awrites=[t_qdf])
                P.dma("s_qb", qa_d[:, 0:64, gsl].rearrange("h d t -> d h t"), qdbs.ap, reads=[qdbs.t], awrites=[t_qdb])
                P.op('dve', 'tensor_copy', dict(out=kdb.ap, in_=kd.ap.rearrange('p h d -> p (h d)')), reads=[kd.t], writes=[kdb.t])
                for hp in range(2):
                    tpv = ps(6 + hp, 0, 64, BF16)
                    transpose_to(kdTs.ap[:, hp, :], [kdTs.t], kdb.ap[:, hp * 128:(hp + 1) * 128], [kdb.t], tpv, tp_t[hp], identb, alt("dve", "act"))
                    ksv = banks[5][:, hp:hp + 1]
                    P.op('pe', 'matmul', dict(out=ksv, lhsT=kd.ap[:, 2 * hp:2 * hp + 2, :].rearrange('p h d -> p (h d)'), rhs=onesf.ap[:, 0:1], start=True, stop=True), reads=[kd.t, onesf.t], writes=[ks_t])
                blk = tl // 2
                if tl % 2 == 0:
                    P.op('dve', 'tensor_copy', dict(out=ksum.ap[:, :, blk], in_=banks[5][:, 0:2]), reads=[ks_t], writes=[ksum.t])
                else:
                    P.op('dve', 'tensor_tensor', dict(out=ksum.ap[:, :, blk], in0=banks[5][:, 0:2], in1=ksum.ap[:, :, blk], op=ALU.add), reads=[ks_t], writes=[ksum.t])
                P.dma("s_kd", pay[tt].ap()[ROW_KD:ROW_KD + 256, ssl].rearrange("(a p) t -> p a t", a=2), kdTs.ap, reads=[kdTs.t], awrites=[t_pay[tt]])
                for hp in range(2):
                    base = hp * 193
                    P.op('act', 'copy', dict(out=vDs.ap[:, sub, base:base + 64], in_=mv_ps[:, 2 * hp * 64:(2 * hp + 1) * 64]), reads=[pT_t[2]], writes=[vDs.t])
                    P.op('act', 'copy', dict(out=vDs.ap[:, sub, base + 129:base + 193], in_=mv_ps[:, (2 * hp + 1) * 64:(2 * hp + 2) * 64]), reads=[pT_t[2]], writes=[vDs.t])
            for (vs_, row0) in ((vAs, ROW_VA), (vDs, ROW_VD)):
                sec = pay[tt].ap()[row0:row0 + 386, :].rearrange("r c -> (r c)")
                for h in range(4):
                    c0 = (0, 65, 193, 258)[h]
                    P.dma("s_v%d" % (row0 == ROW_VD), sec[VOFF[h]:VOFF[h] + 128 * 4 * VW[h]].rearrange("(p t w) -> p t w", p=128, t=4),
                          vs_.ap[:, :, c0:c0 + VW[h]], reads=[vs_.t], awrites=[t_pay[tt]], seal=False)
                P.seal("s_v%d" % (row0 == ROW_VD))
            P.coll("cc", lambda g, tt=tt: g.collective_compute("AllGather", ALU.bypass, replica_groups=[list(range(NCORES))],
                                                              ins=[pay[tt].ap().opt()], outs=[gat[tt].ap().opt()]),
                   reads=[t_pay[tt]], writes=[t_gat[tt]])
        P.op('dve', 'tensor_scalar', dict(out=ksum.ap.rearrange('p a b -> p (a b)'), in0=ksum.ap.rearrange('p a b -> p (a b)'), scalar1=1.0 / 256, scalar2=None, op0=ALU.mult), reads=[], writes=[ksum.t])
        P.dma("pf", payf.ap()[:, 0:8].rearrange("(a p) b -> p a b", a=2), ksum.ap, reads=[ksum.t], awrites=[t_payf])
        P.barrier()


        apos[0] = mark_T
        WF = sb([128, 8, NF_COLS], BF16)
        hnT = sb([128, 8, 512], BF16)
        chs = sb([128, 512]); zs = sb([128, 512]); cv = sb([128, 512]); sgt = sb([128, 512]); cbg = sb([128, 512])
        t1s = sb([128, 512]); t2s = sb([128, 512])
        wi = 0
        for kc in range(8):
            for hf in range(2):
                s_ = wst[wi % 2]
                P.dma(f"ws{wi % 2}", s_.ap[:, 0:1024], W["wF"][kc * 128:(kc + 1) * 128, hf * 1024:(hf + 1) * 1024], writes=[s_.t])
                if wi % 2 == 0:
                    P.op('dve', 'tensor_scalar', dict(out=WF.ap[:, kc, hf * 1024:(hf + 1) * 1024], in0=s_.ap[:, 0:1024], scalar1=ngs.ap[:, kc:kc + 1], scalar2=None, op0=ALU.mult), reads=[s_.t, ngs.t], writes=[WF.t])
                else:
                    P.op('act', 'activation', dict(out=WF.ap[:, kc, hf * 1024:(hf + 1) * 1024], in_=s_.ap[:, 0:1024], func=AF.Copy, scale=ngs.ap[:, kc:kc + 1]), reads=[s_.t, ngs.t], writes=[WF.t])
                wi += 1
        pf_t = [bk[0], bk[1], bk[2], bk[3]]
        pfi = [0]
        for tt in range(4):
            tsl = slice(tt * 512, (tt + 1) * 512)
            for kc in range(8):
                P.op(alt('dve', 'pool'), 'tensor_tensor', dict(out=hnT.ap[:, kc, :], in0=xT.ap[:, kc, tsl], in1=rstdbc.ap[:, tsl], op=ALU.mult), reads=[xT_t[kc][tt], rstd_t[tt]], writes=[hnT.t])

            def proj(j):
                b = pfi[0] % 4
                pfi[0] += 1
                pv = ps(b, 0, 512)
                for kc in range(8):
                    P.op('pe', 'matmul', dict(out=pv, lhsT=WF.ap[:, kc, j * 128:(j + 1) * 128], rhs=hnT.ap[:, kc, :], start=kc == 0, stop=kc == 7), reads=[WF.t, hnT.t], writes=[pf_t[b]])
                return pv, pf_t[b]

            z3 = zs.ap.rearrange("p (b t) -> p b t", b=2); c3 = cv.ap.rearrange("p (b t) -> p b t", b=2); g3 = cbg.ap.rearrange("p (b t) -> p b t", b=2)
            blk0 = tt * 2
            for i in range(2):
                pA, tA = proj(4 * i + 0)
                P.op('act', 'copy', dict(out=chs.ap, in_=pA), reads=[tA], writes=[chs.t])
                pB, tB = proj(4 * i + 1)
                P.op('dve', 'tensor_tensor', dict(out=zs.ap, in0=pB, in1=chs.ap, op=ALU.mult), reads=[tB, chs.t], writes=[zs.t])
                P.op('act', 'activation', dict(out=cv.ap, in_=zs.ap, func=AF.Copy, scale=convw.ap[:, i, 2:3]), reads=[zs.t, convw.t], writes=[cv.t])
                P.op('dve', 'scalar_tensor_tensor', dict(out=c3[:, :, 1:256], in0=z3[:, :, 0:255], scalar=convw.ap[:, i, 1:2], in1=c3[:, :, 1:256], op0=ALU.mult, op1=ALU.add), reads=[zs.t, convw.t], writes=[cv.t])
                P.op('dve', 'scalar_tensor_tensor', dict(out=c3[:, :, 2:256], in0=z3[:, :, 0:254], scalar=convw.ap[:, i, 0:1], in1=c3[:, :, 2:256], op0=ALU.mult, op1=ALU.add), reads=[zs.t, convw.t], writes=[cv.t])
                P.op('pool', 'tensor_copy', dict(out=ztl.ap[:, i, blk0:blk0 + 2, :], in_=z3[:, :, 254:256]), reads=[zs.t], writes=[ztl.t])
                P.op('pool', 'tensor_copy', dict(out=c0s.ap[:, i, blk0:blk0 + 2, :], in_=c3[:, :, 0:2]), reads=[cv.t], writes=[c0s.t])
                pC, tC = proj(4 * i + 2)
                pD, tD = proj(4 * i + 3)
                P.op('act', 'activation', dict(out=sgt.ap, in_=pD, func=AF.Silu), reads=[tD], writes=[sgt.t])
                P.op('dve', 'tensor_tensor', dict(out=cbg.ap, in0=pC, in1=sgt.ap, op=ALU.mult), reads=[tC, sgt.t], writes=[cbg.t])
                P.op('pool', 'tensor_copy', dict(out=cbgs.ap[:, i, blk0:blk0 + 2, :], in_=g3[:, :, 0:2]), reads=[cbg.t], writes=[cbgs.t])
                P.op('pool', 'tensor_tensor', dict(out=YT.ap[:, 2 + i, tsl], in0=cv.ap, in1=cbg.ap, op=ALU.mult), reads=[cv.t, cbg.t], writes=[YT_t[2 + i][tt]])
            for i in range(2):
                b = pfi[0] % 4
                pfi[0] += 1
                pE = ps(b, 0, 512); tE = pf_t[b]
                for sub in range(4):
                    tl = tt * 4 + sub
                    for hf in range(2):
                        g = 2 * i + hf
                        pEh = banks[b][hf * 64:(hf + 1) * 64, sub * 128:(sub + 1) * 128]
                        P.op('pe', 'matmul', dict(out=pEh, lhsT=vn[tl].ap[:, g * 64:(g + 1) * 64], rhs=wsT.ap[:, g, :], start=True, stop=True, tile_position=(0, 64 * hf)), reads=[vn[tl].t, wsT.t], writes=[tE])
                pF, tF = proj(8 + 2 * i)
                P.op('dve', 'tensor_tensor', dict(out=t1s.ap.rearrange('p (c t) -> p c t', c=4), in0=pE.rearrange('p (c t) -> p c t', c=4), in1=bsbc.ap[:, i, :].unsqueeze(1).to_broadcast([128, 4, 128]), op=ALU.add), reads=[tE, bsbc.t], writes=[t1s.t])
                P.op('dve', 'tensor_tensor', dict(out=t2s.ap, in0=pF, in1=t1s.ap, op=ALU.mult), reads=[tF, t1s.t], writes=[t2s.t])
                pG, tG = proj(9 + 2 * i)
                P.op('act', 'activation', dict(out=sgt.ap, in_=pG, func=AF.Silu), reads=[tG], writes=[sgt.t])
                P.op('pool', 'tensor_tensor', dict(out=YT.ap[:, 4 + i, tsl], in0=t2s.ap, in1=sgt.ap, op=ALU.mult), reads=[t2s.t, sgt.t], writes=[YT_t[4 + i][tt]])
            for (j, ych) in ((12, 0), (13, 1), (14, 6), (15, 7)):
                pH, tH = proj(j)
                P.op('act', 'activation', dict(out=YT.ap[:, ych, tsl], in_=pH, func=AF.Silu), reads=[tH], writes=[YT_t[ych][tt]])
        P.dma("pf", payf.ap()[:, 8:24].rearrange("(a p) (b j) -> p a b j", a=2, j=2), ztl.ap, reads=[ztl.t], awrites=[t_payf])
        P.coll("cc", lambda g: g.collective_compute("AllGather", ALU.bypass, replica_groups=[list(range(NCORES))],
                                                    ins=[payf.ap().opt()], outs=[gatf.ap().opt()]),
               reads=[t_payf], writes=[t_gatf])
        P.barrier()

        apos[0] = mark_T
        gf3 = gatf.ap().rearrange("(r q) c -> q r c", r=NCORES)
        tails = sb([128, 2, 64, 2]); zh = sb([128, 2, 8, 2]); tmpf = sb([128, 2, 64, 2]); dlt = sb([128, 2, 8, 2])
        for a in range(2):
            P.dma("fx", tails.ap[:, a].rearrange("p (r b) j -> p r (b j)", r=8), gf3[a * 128:(a + 1) * 128, :, 8:24], reads=[t_gatf], awrites=[tails.t], seal=False)
        P.seal("fx")
        for a in range(8):
            P.op('dve', 'tensor_tensor', dict(out=tmpf.ap, in0=tails.ap, in1=mtab.ap[:, 2, a, :].unsqueeze(1).unsqueeze(3).to_broadcast([128, 2, 64, 2]), op=ALU.mult), reads=[tails.t, mtab.t], writes=[tmpf.t])
            P.op('dve', 'tensor_reduce', dict(out=zh.ap[:, :, a, :], in_=tmpf.ap.rearrange('p c s j -> p c j s'), axis=AX.X, op=ALU.add), reads=[tmpf.t], writes=[zh.t])
        for i in range(2):
            P.op('dve', 'tensor_scalar', dict(out=dlt.ap[:, i, :, 0], in0=zh.ap[:, i, :, 1], scalar1=convw.ap[:, i, 1:2], scalar2=None, op0=ALU.mult), reads=[zh.t, convw.t], writes=[dlt.t])
            P.op('dve', 'scalar_tensor_tensor', dict(out=dlt.ap[:, i, :, 0], in0=zh.ap[:, i, :, 0], scalar=convw.ap[:, i, 0:1], in1=dlt.ap[:, i, :, 0], op0=ALU.mult, op1=ALU.add), reads=[zh.t], writes=[dlt.t])
            P.op('dve', 'tensor_scalar', dict(out=dlt.ap[:, i, :, 1], in0=zh.ap[:, i, :, 1], scalar1=convw.ap[:, i, 0:1], scalar2=None, op0=ALU.mult), reads=[zh.t], writes=[dlt.t])
            P.op('dve', 'tensor_tensor', dict(out=dlt.ap[:, i], in0=dlt.ap[:, i], in1=c0s.ap[:, i], op=ALU.add), reads=[c0s.t], writes=[dlt.t])
            P.op('dve', 'tensor_tensor', dict(out=YT.ap[:, 2 + i, :].rearrange('p (b t) -> p b t', b=8)[:, :, 0:2], in0=dlt.ap[:, i], in1=cbgs.ap[:, i], op=ALU.mult), reads=[dlt.t, cbgs.t], writes=[YT_t[2 + i][0], YT_t[2 + i][1], YT_t[2 + i][2], YT_t[2 + i][3]])

        kmT = sb([64, 4, 64]); qdf = sb([64, TOK]); pbt = sb([128, 8, 64])
        btile = [sb([128, 128], BF16) for _ in range(2)]
        bTs = [sb([128, 512], BF16) for _ in range(2)]
        gm = sb([128, 64]); g1 = sb([128, 64]); mx8 = sb([128, 8]); sel = sb([128, 64])
        for bt_ in btile:
            P.op('pool', 'memset', dict(ap=bt_.ap, constant=0.0), writes=[bt_.t])
        P.op('dve', 'tensor_scalar', dict(out=pbt.ap, in0=mtab.ap[:, 0], scalar1=1e+30, scalar2=-1e+30, op0=ALU.mult, op1=ALU.add), reads=[mtab.t], writes=[pbt.t])
        for h in range(4):
            P.dma("fk", kmT.ap[:, h, :].rearrange("p (r b) -> p r b", r=8), gf3[h * 64:(h + 1) * 64, :, 0:8], reads=[t_gatf], awrites=[kmT.t], seal=False)
        P.seal("fk")
        gate_t, tpb_t = bk[6], bk[7]
        gate_ps = ps(6, 0, 64)
        tpb = ps(7, 0, 64, BF16)
        bi = 0
        for h in range(4):
            P.dma("qf", qdf.ap, qdf_d[h], reads=[t_qdf], writes=[qdf.t])
            for tl in range(16):
                a = tl // 2
                bt_ = btile[bi % 2]
                bs_ = bTs[(bi // 4) % 2]
                bi += 1
                P.op('pe', 'matmul', dict(out=gate_ps, lhsT=qdf.ap[:, tl * 128:(tl + 1) * 128], rhs=kmT.ap[:, h, :], start=True, stop=True), reads=[qdf.t, kmT.t], writes=[gate_t])
                P.op('dve', 'tensor_tensor', dict(out=g1.ap, in0=gate_ps, in1=mtab.ap[:, 0, a, :], op=ALU.mult), reads=[gate_t, mtab.t], writes=[g1.t])
                P.op('dve', 'tensor_tensor', dict(out=gm.ap, in0=g1.ap, in1=pbt.ap[:, a, :], op=ALU.add), reads=[g1.t, pbt.t], writes=[gm.t])
                P.op('dve', 'max', dict(out=mx8.ap, in_=gm.ap), reads=[gm.t], writes=[mx8.t])
                P.op('dve', 'scalar_tensor_tensor', dict(out=sel.ap, in0=gm.ap, scalar=mx8.ap[:, 2:3], in1=mtab.ap[:, 0, a, :], op0=ALU.is_ge, op1=ALU.mult), reads=[gm.t, mx8.t], writes=[sel.t])
                P.op('dve', 'tensor_tensor', dict(out=sel.ap, in0=sel.ap, in1=mtab.ap[:, 1, a, :], op=ALU.add), reads=[mtab.t], writes=[sel.t])
                P.op('dve', 'tensor_scalar', dict(out=bt_.ap[:, 64:128], in0=sel.ap, scalar1=-NEGB, scalar2=NEGB, op0=ALU.mult, op1=ALU.add), reads=[sel.t], writes=[bt_.t])
                P.op('pe', 'transpose', dict(out=tpb, in_=bt_.ap, identity=identb.ap), reads=[bt_.t, identb.t], writes=[tpb_t])
                P.op('act', 'copy', dict(out=bs_.ap[64:128, tl % 4 * 128:(tl % 4 + 1) * 128], in_=tpb[64:128, :]), reads=[tpb_t], writes=[bs_.t])
                if tl % 4 == 3:
                    tt = tl // 4
                    P.dma("qb%d" % ((bi - 1) // 4 % 2), qa_d[h, 64:128, tt * 512:(tt + 1) * 512], bs_.ap[64:128, :], reads=[bs_.t], awrites=[t_qdb])
        P.barrier()

        apos[0] = mark_T
        Kc = [sb([128, 8, 512], BF16) for _ in range(2)]
        Vc = [sb([128, 8, 4, 128], BF16) for _ in range(2)]
        QT = [sb([128, TOK], BF16) for _ in range(2)]
        NPT = 4
        pTt = [sb([128, 512], BF16) for _ in range(NPT)]
        rd = sb([128, 512]); ftmp = sb([128, 512]); mskt = [sb([128, 256], BF16) for _ in range(2)]
        woutb = sb([128, 8, DM], BF16)
        for kc in range(8):
            s_ = wst[kc % 2]
            P.dma(f"ws{kc % 2}", s_.ap[:, 0:1024], W["wout"][kc * 128:(kc + 1) * 128, :], writes=[s_.t])
            P.op("dve", "tensor_copy", dict(out=woutb.ap[:, kc, :], in_=s_.ap[:, 0:1024]), reads=[s_.t], writes=[woutb.t])
        gat3 = [gat[m].ap().rearrange("(r q) c -> q r c", r=NCORES) for m in range(4)]
        gatflat = [gat[m].ap().rearrange("(r q) c -> r (q c)", r=NCORES) for m in range(4)]
        LAG = 2
        si = 0
        ci = 0
        mi = 0
        for hh in range(8):
            kind, h = hh // 4, hh % 4
            KD = 96 if kind == 0 else 128
            scale = float(96 ** -0.5) if kind == 0 else 0.125
            W_ = VW[h]
            q_ = QT[hh % 2]
            if kind == 0:
                P.dma(f"qt{hh % 2}", q_.ap[0:96, :], qT_d[h], reads=[t_qT], writes=[q_.t])
            else:
                P.dma(f"qt{hh % 2}", q_.ap, qa_d[h], reads=[t_qdb], writes=[q_.t])
            ych = kind * 6 + h // 2
            orow = slice((h % 2) * 64, (h % 2) * 64 + 64)
            dp = 64 if h % 2 == 0 else 0
            pend = []

            def finalize(j):
                accj = banks[j]
                P.op("dve", "reciprocal", dict(out=rd.ap[dp:dp + 1, :], in_=accj[dp:dp + 1, :]), reads=[bk[j]], writes=[rd.t])
                P.op("pe", "matmul", dict(out=banks[7][:, :], lhsT=onesf.ap[dp:dp + 1, :], rhs=rd.ap[dp:dp + 1, :], start=True, stop=True), reads=[rd.t, onesf.t], writes=[bk[7]])
                ysl = YT.ap[orow, ych, j * 512:(j + 1) * 512]
                P.op("dve", "tensor_tensor", dict(out=ftmp.ap[orow, :], in0=banks[7][orow, :], in1=ysl, op=ALU.mult), reads=[bk[7], YT_t[ych][j]], writes=[ftmp.t])
                P.op("dve", "tensor_tensor", dict(out=ysl, in0=accj[orow, :], in1=ftmp.ap[orow, :], op=ALU.mult), reads=[bk[j], ftmp.t], writes=[YT_t[ych][j]])

            def emit_pv(it):
                (j, qs, vap, pt, first, last, vtok) = it
                P.op("pe", "matmul", dict(out=banks[j][0:vap.shape[1], qs], lhsT=vap, rhs=pt.ap[:, qs], start=first, stop=last), reads=[pt.t, vtok], writes=[bk[j]])
                if last:
                    finalize(j)

            for m in range(4):
                kc_ = Kc[ci % 2]; vc_ = Vc[ci % 2]
                chn = ci % 2
                ci += 1
                if kind == 0:
                    P.dma(f"kc{chn}", kc_.ap[0:64], gat3[m][ROW_KN + h * 64:ROW_KN + (h + 1) * 64], reads=[t_gat[m]], writes=[kc_.t], seal=False)
                    P.dma(f"kc{chn}", kc_.ap[64:96], gat3[m][ROW_KR:ROW_KR + 32], reads=[t_gat[m]], awrites=[kc_.t], seal=False)
                    row0 = ROW_VA
                else:
                    P.dma(f"kc{chn}", kc_.ap[0:64], gat3[m][ROW_KD + h * 64:ROW_KD + (h + 1) * 64], reads=[t_gat[m]], writes=[kc_.t], seal=False)
                    P.dma(f"kc{chn}", kc_.ap[64:128], onehot_in[m].rearrange("j (r t) -> j r t", r=8), awrites=[kc_.t], seal=False)
                    row0 = ROW_VD
                P.seal(f"kc{chn}")
                for r in range(8):
                    vsrc = gatflat[m][r, row0 * 512 + VOFF[h]:row0 * 512 + VOFF[h] + 128 * 4 * W_].rearrange("(p t w) -> p t w", p=128, t=4)
                    if r == 0:
                        P.dma(f"vc{chn}", vc_.ap[:, r, :, 0:W_], vsrc, reads=[t_gat[m]], writes=[vc_.t], seal=False)
                    else:
                        P.dma(f"vc{chn}", vc_.ap[:, r, :, 0:W_], vsrc, reads=[t_gat[m]], awrites=[vc_.t], seal=False)
                P.seal(f"vc{chn}")
                for b in (2 * m, 2 * m + 1):
                    e_ = b % 2
                    for r in range(8):
                        for kt in range(2):
                            kap = kc_.ap[0:KD, r, (b % 2) * 256 + kt * 128:(b % 2) * 256 + (kt + 1) * 128]
                            vap = vc_.ap[:, r, (b % 2) * 2 + kt, 0:W_]
                            for j in range(b // 2, 4):
                                diag = (j == b // 2)
                                qs = slice(256, 512) if (diag and e_ == 1) else slice(0, 512)
                                first = (b == 0 and r == 0 and kt == 0) or (qs.start == 256 and False)
                                last = (b == 2 * j + 1 and r == 7 and kt == 1)
                                sb_ = 4 + si % 3
                                pt = pTt[si % NPT]
                                si += 1
                                P.op("pe", "matmul", dict(out=banks[sb_][:, qs], lhsT=kap, rhs=q_.ap[0:KD, j * 512 + qs.start:j * 512 + qs.stop], start=True, stop=True),
                                     reads=[kc_.t, q_.t], writes=[bk[sb_]])
                                if diag:
                                    dq = slice(0, 256) if e_ == 0 else slice(256, 512)
                                    P.op("act", "activation", dict(out=pt.ap[:, dq], in_=banks[sb_][:, dq], func=AF.Exp, scale=scale,
                                                                   bias=ctab.ap[:, 16 + e_ * 8 + r:16 + e_ * 8 + r + 1]),
                                         reads=[ctab.t], writes=[bk[sb_], pt.t])
                                    if e_ == 0:
                                        P.op("act", "activation", dict(out=pt.ap[:, 256:512], in_=banks[sb_][:, 256:512], func=AF.Exp, scale=scale), writes=[bk[sb_], pt.t])
                                    mk_ = mskt[mi % 2]
                                    mi += 1
                                    P.op("dve", "tensor_scalar", dict(out=mk_.ap, in0=tri.ap[:, kt, :], scalar1=ctab.ap[:, e_ * 8 + r:e_ * 8 + r + 1], scalar2=None, op0=ALU.max),
                                         reads=[tri.t, ctab.t], writes=[mk_.t])
                                    P.op("dve", "tensor_tensor", dict(out=pt.ap[:, dq], in0=pt.ap[:, dq], in1=mk_.ap, op=ALU.mult), reads=[mk_.t], writes=[pt.t])
                                else:
                                    P.op("act", "activation", dict(out=pt.ap, in_=banks[sb_][:, :], func=AF.Exp, scale=scale), writes=[bk[sb_], pt.t])
                                pend.append((j, qs, vap, pt, first, last, vc_.t))
                                if len(pend) > LAG:
                                    emit_pv(pend.pop(0))
            while pend:
                emit_pv(pend.pop(0))
        P.barrier()

        po_t = [bk[4], bk[5], bk[6], bk[7]]
        oi = 0
        for nt in range(4):
            tsl = slice(nt * 512, (nt + 1) * 512)
            for mc in range(8):
                b = 4 + oi % 4
                oi += 1
                pv = ps(b, 0, 512)
                for kc in range(8):
                    P.op('pe', 'matmul', dict(out=pv, lhsT=woutb.ap[:, kc, mc * 128:(mc + 1) * 128], rhs=YT.ap[:, kc, tsl], start=kc == 0, stop=kc == 7), reads=[woutb.t, YT_t[kc][nt]], writes=[po_t[b - 4]])
                P.op('dve', 'tensor_tensor', dict(out=xT.ap[:, mc, tsl], in0=pv, in1=xT.ap[:, mc, tsl], op=ALU.add), reads=[po_t[b - 4]], writes=[xT_t[mc][nt]])
                if last_layer:
                    final_ops.append(P.dma("out", out_T[mc * 128:(mc + 1) * 128, tsl], xT.ap[:, mc, tsl], reads=[xT_t[mc][nt]], writes=[Tok()], seal=False))
        P.barrier()
        apos[0] = mark_l

    P.emit(final_waits=final_ops)
    return nc


def _core_tables(c):
    alpha = np.zeros((2, 8), np.float32)
    nbias = np.zeros((2, 8), np.float32)
    for e in range(2):
        for r in range(8):
            mine = c if e == 0 else 7 - c
            other = r if e == 0 else 7 - r
            if other < mine:
                alpha[e, r] = 1.0
            elif other > mine:
                nbias[e, r] = NEGB
    ctab = np.zeros((1, 32), np.float32)
    ctab[0, 0:16] = alpha.reshape(-1)
    ctab[0, 16:32] = nbias.reshape(-1)
    past = np.zeros((8, 64), np.float32)
    own = np.zeros((8, 64), np.float32)
    selm = np.zeros((8, 64), np.float32)
    for a in range(8):
        gq = gblock(c, a)
        for r in range(8):
            for b in range(8):
                gj = gblock(r, b)
                j = r * 8 + b
                if gj < gq:
                    past[a, j] = 1.0
                if gj == gq:
                    own[a, j] = 1.0
                if gj == gq - 1:
                    selm[a, j] = 1.0
    mtab = np.concatenate([past.reshape(-1), own.reshape(-1), selm.reshape(-1)])[None, :].astype(np.float32)
    return ctab, mtab


def _shared_consts():
    ident = np.eye(128, dtype=np.float32)
    p = np.arange(128)[:, None]
    q = np.arange(128)[None, :]
    tr = (q >= p).astype(np.float32)
    tri = np.zeros((128, 2, 256), np.float32)
    tri[:, 0, 0:128] = tr
    tri[:, 0, 128:256] = 1.0
    tri[:, 1, 128:256] = tr
    trilm = (p <= q).astype(np.float32)
    invA = np.power(np.float32(THETA), -np.arange(16, dtype=np.float32) * np.float32(2.0) / np.float32(32.0)).astype(np.float32)
    invD = np.power(np.float32(THETA), -np.arange(8, dtype=np.float32) * np.float32(2.0) / np.float32(16.0)).astype(np.float32)
    invf = np.concatenate([invA, invD])[None, :].astype(np.float32)
    onehot = np.zeros((4, 64, 4096), np.float32)
    for m in range(4):
        for r in range(8):
            for b in (2 * m, 2 * m + 1):
                onehot[m, r * 8 + b, r * 512 + (b % 2) * 256:r * 512 + (b % 2 + 1) * 256] = 1.0
    return ident, tri, trilm, invf, onehot.astype(ml_dtypes.bfloat16)


def _layer_inputs(li, l, inp):
    w_in = np.asarray(inp["w_in"][l])
    tcols = np.concatenate([np.arange(OFF[n], OFF[n] + T_W[n]) for n in T_ORDER])
    fcols = np.concatenate([np.arange(OFF[n] + i * 128, OFF[n] + (i + 1) * 128) for (n, i) in F_BLOCKS])
    d = {}
    d[f"wT{li}"] = np.ascontiguousarray(w_in[:, tcols])
    d[f"wF{li}"] = np.ascontiguousarray(w_in[:, fcols])
    d[f"ng{li}"] = np.ascontiguousarray(np.asarray(inp["norm_g"][l]).reshape(8, 128).T)
    d[f"qng{li}"] = np.ascontiguousarray(np.asarray(inp["mla_q_norm_g"][l]).reshape(2, 128).T)
    d[f"kvng{li}"] = np.ascontiguousarray(np.asarray(inp["mla_kv_norm_g"][l]).reshape(1, 128).T)
    d[f"wuq{li}"] = np.ascontiguousarray(inp["mla_w_uq"][l])
    d[f"wukv{li}"] = np.ascontiguousarray(inp["mla_w_ukv"][l])
    d[f"gvec{li}"] = np.concatenate([np.asarray(inp[k][l]).reshape(-1) for k in
                                     ("mla_q_g", "mla_k_nope_g", "mla_k_rope_g", "moba_q_g", "moba_k_g", "sg_ln_g", "sg_ln_b")])[None, :].astype(np.float32)
    d[f"convw{li}"] = np.ascontiguousarray(np.transpose(np.asarray(inp["conv_w"][l]).reshape(3, 2, 128), (2, 1, 0)))
    d[f"wsT{li}"] = np.ascontiguousarray(np.transpose(np.asarray(inp["sg_w"][l]), (2, 0, 1)))
    d[f"sgb{li}"] = np.ascontiguousarray(np.asarray(inp["sg_b"][l]).reshape(1, 512))
    d[f"wout{li}"] = np.ascontiguousarray(inp["w_out"][l])
    return {k: np.asarray(v, dtype=np.float32) for k, v in d.items()}


_NC_CACHE = {}


def run_layers(x_full, inputs, layers):
    key = tuple(range(len(layers)))
    if key not in _NC_CACHE:
        _NC_CACHE[key] = build(layers=key)
    nc = _NC_CACHE[key]
    ident, tri, trilm, invf, onehot = _shared_consts()
    pos = np.asarray(inputs["positions"]).reshape(-1).astype(np.int32)
    lay = {}
    for li, l in enumerate(layers):
        lay.update(_layer_inputs(li, l, inputs))
    in_maps = []
    idxs = []
    for c in range(NCORES):
        idx = np.concatenate([np.arange(256 * g, 256 * (g + 1)) for g in owned_blocks(c)])
        idxs.append(idx)
        ctab, mtab = _core_tables(c)
        m = dict(lay)
        m["xT"] = np.ascontiguousarray(x_full[idx, :].T)
        m["pos"] = np.ascontiguousarray(pos[idx].reshape(16, 128).T)
        m.update(ident=ident, tri=tri, trilm=trilm, invf=invf, ctab=ctab, mtab=mtab, onehot=onehot)
        in_maps.append(m)
    res = run_bass_kernel_spmd(nc, in_maps, core_ids=list(range(NCORES)))
    out = np.empty((SEQ, DM), np.float32)
    for c in range(NCORES):
        out[idxs[c], :] = res.results[c]["outT"].T
    return out


def kernel(**inputs):
    x = np.asarray(inputs["x"], dtype=np.float32)[0]
    out = run_layers(x, inputs, (0, 1))
    return out[None, :, :].astype(np.float32)
```
